# Optimizing a Trainium2 kernel written in Bass

```python
import math
import jax, jax.numpy as jnp
from jax import lax
import numpy as np

D_MODEL = 1024
BATCH = 16
SEQ = 2048
DEPTH = 1

CHUNK = 64
Q_BLOCK = 128
ROPE_THETA = 500000.0
ROPE_FRACTION = 4
LN_EPS = 1e-5
A_HEADS = 8
A_HEAD_DIM = 64
A_WIDTH = A_HEADS * A_HEAD_DIM
IDX_HEADS = 8
IDX_DIM = 32
TOPK_MAX = 256
B_HEADS = 4
B_HEAD_DIM = 64
B_WIDTH = B_HEADS * 2 * B_HEAD_DIM
MEM_LEN = 256
C_HEADS = 4
C_HEAD_DIM = 128
C_WIDTH = C_HEADS * C_HEAD_DIM
N_BRANCH = 3
DEEPNORM_ALPHA = (2.0 * DEPTH) ** 0.25
DEEPNORM_BETA = (8.0 * DEPTH) ** -0.25
IN_SPLITS = (
    A_HEADS * A_HEAD_DIM,
    A_HEAD_DIM,
    A_HEAD_DIM,
    A_WIDTH,
    IDX_HEADS * IDX_DIM,
    IDX_DIM,
    IDX_HEADS,
    B_WIDTH,
    B_WIDTH,
    B_WIDTH,
    B_WIDTH,
    C_WIDTH,
    C_WIDTH,
    N_BRANCH * D_MODEL,
)
VALUE_SPLITS = (2, 9)
IN_WIDTH = sum(IN_SPLITS)

kernel_name = 'hybrid_dsa_diffattn_memxattn_deepnorm'


def _split_cols(h, sizes):
    offs = np.cumsum(sizes)[:-1].tolist()
    return jnp.split(h, offs, axis=-1)


def _layer_norm(x, g, b):
    xf = x.astype(jnp.float32)
    mu = jnp.mean(xf, -1, keepdims=True)
    var = jnp.mean(jnp.square(xf - mu), -1, keepdims=True)
    y = (xf - mu) * lax.rsqrt(var + LN_EPS) * g.astype(jnp.float32) + b.astype(jnp.float32)
    return y.astype(x.dtype)


def _partial_rope(x, pos):
    d = x.shape[-1]
    r = d // ROPE_FRACTION
    half = r // 2
    inv = jnp.power(ROPE_THETA, -jnp.arange(half, dtype=jnp.float32) * (2.0 / r))
    ang = pos[:, None] * inv[None, :]
    cos = jnp.cos(ang)[:, None, :]
    sin = jnp.sin(ang)[:, None, :]
    xf = x.astype(jnp.float32)
    x1, x2, xp = xf[..., :half], xf[..., half:r], xf[..., r:]
    out = jnp.concatenate([x1 * cos - x2 * sin, x2 * cos + x1 * sin, xp], axis=-1)
    return out.astype(x.dtype)


def _chunk_limit(t):
    return (t // CHUNK + 1) * CHUNK


def _to_blocks(a, nb):
    a = a.reshape((a.shape[0], nb, Q_BLOCK) + a.shape[2:])
    return jnp.moveaxis(a, 1, 0)


def _from_blocks(a):
    a = jnp.moveaxis(a, 0, 1)
    return a.reshape((a.shape[0], a.shape[1] * a.shape[2]) + a.shape[3:])


def _dsa_attention(q, k, v, iq, ik, iw):
    seq = q.shape[1]
    nb = seq // Q_BLOCK
    n_sel = min(TOPK_MAX, seq // 4)
    key_pos = jnp.arange(seq)
    scale = A_HEAD_DIM ** -0.5

    def block(args):
        j, qb, iqb, iwb = args
        limit = _chunk_limit(j * Q_BLOCK + jnp.arange(Q_BLOCK))
        rel = jax.nn.relu(jnp.einsum('bqhd,bsd->bqhs', iqb, ik).astype(jnp.float32))
        score = jnp.einsum('bqh,bqhs->bqs', iwb.astype(jnp.float32), rel)
        score = jnp.where(key_pos[None, None, :] < limit[None, :, None], score, -jnp.inf)
        _, idx = lax.top_k(score, n_sel)
        ok = idx < limit[None, :, None]
        k_sel = jax.vmap(lambda kk, ii: kk[ii])(k, idx)
        v_sel = jax.vmap(lambda vv, ii: vv[ii])(v, idx)
        logits = jnp.einsum('bqhd,bqkd->bqhk', qb, k_sel).astype(jnp.float32) * scale
        logits = jnp.where(ok[:, :, None, :], logits, -jnp.inf)
        p = jax.nn.softmax(logits, axis=-1).astype(v.dtype)
        return jnp.einsum('bqhk,bqkd->bqhd', p, v_sel)

    out = lax.map(block, (jnp.arange(nb), _to_blocks(q, nb), _to_blocks(iq, nb), _to_blocks(iw, nb)))
    return _from_blocks(out)


def _diff_attention(q, k, v, lam):
    seq = q.shape[1]
    nb = seq // Q_BLOCK
    key_pos = jnp.arange(seq)
    scale = B_HEAD_DIM ** -0.5

    def block(args):
        j, qb = args
        limit = _chunk_limit(j * Q_BLOCK + jnp.arange(Q_BLOCK))
        logits = jnp.einsum('bqhcd,bshcd->bhcqs', qb, k).astype(jnp.float32) * scale
        logits = jnp.where(key_pos[None, :] < limit[:, None], logits, -jnp.inf)
        p = jax.nn.softmax(logits, axis=-1)
        a = (p[:, :, 0] - lam * p[:, :, 1]).astype(v.dtype)
        return jnp.einsum('bhqs,bshe->bqhe', a, v)

    out = lax.map(block, (jnp.arange(nb), _to_blocks(q, nb)))
    return _from_blocks(out)


def _memory_attention(q, mk, mv):
    logits = jnp.einsum('bqhd,bmhd->bhqm', q, mk).astype(jnp.float32) * (C_HEAD_DIM ** -0.5)
    p = jax.nn.softmax(logits, axis=-1).astype(mv.dtype)
    return jnp.einsum('bhqm,bmhd->bqhd', p, mv)


def setup_inputs(seed: int = 0) -> dict:
    key = jax.random.key(seed)
    ks = jax.random.split(key, 16)
    f32 = jnp.float32
    x = jax.random.normal(ks[0], (BATCH, SEQ, D_MODEL), f32)
    mem = jax.random.normal(ks[1], (BATCH, MEM_LEN, D_MODEL), f32)
    ln_in_g = 1.0 + 0.02 * jax.random.normal(ks[2], (D_MODEL,), f32)
    ln_in_b = 0.02 * jax.random.normal(ks[3], (D_MODEL,), f32)
    col_scale = jnp.concatenate([jnp.full((s,), DEEPNORM_BETA if i in VALUE_SPLITS else 1.0, f32)
                                 for i, s in enumerate(IN_SPLITS)])
    w_in = jax.random.normal(ks[4], (DEPTH, D_MODEL, IN_WIDTH), f32) * (D_MODEL ** -0.5) * col_scale
    kv_scale = jnp.concatenate([jnp.ones((C_WIDTH,), f32), jnp.full((C_WIDTH,), DEEPNORM_BETA, f32)])
    w_mem_kv = jax.random.normal(ks[5], (DEPTH, D_MODEL, 2 * C_WIDTH), f32) * (D_MODEL ** -0.5) * kv_scale
    diff_lambda = 0.1 * jax.random.normal(ks[6], (DEPTH, 4, B_HEAD_DIM), f32)
    diff_norm_g = 1.0 + 0.02 * jax.random.normal(ks[7], (DEPTH, 2 * B_HEAD_DIM), f32)
    w_proj_a = jax.random.normal(ks[8], (DEPTH, A_WIDTH, D_MODEL), f32) * (A_WIDTH ** -0.5) * DEEPNORM_BETA
    w_proj_b = jax.random.normal(ks[9], (DEPTH, B_WIDTH, D_MODEL), f32) * (B_WIDTH ** -0.5) * DEEPNORM_BETA
    w_proj_c = jax.random.normal(ks[10], (DEPTH, C_WIDTH, D_MODEL), f32) * (C_WIDTH ** -0.5) * DEEPNORM_BETA
    w_out = jax.random.normal(ks[11], (DEPTH, D_MODEL, D_MODEL), f32) * (D_MODEL ** -0.5) * DEEPNORM_BETA
    ln_g = 1.0 + 0.02 * jax.random.normal(ks[12], (DEPTH, D_MODEL), f32)
    ln_b = 0.02 * jax.random.normal(ks[13], (DEPTH, D_MODEL), f32)
    return {'x': x, 'mem': mem, 'ln_in_g': ln_in_g, 'ln_in_b': ln_in_b, 'w_in': w_in,
            'w_mem_kv': w_mem_kv, 'diff_lambda': diff_lambda, 'diff_norm_g': diff_norm_g,
            'w_proj_a': w_proj_a, 'w_proj_b': w_proj_b, 'w_proj_c': w_proj_c, 'w_out': w_out,
            'ln_g': ln_g, 'ln_b': ln_b}


def reference(x, mem, ln_in_g, ln_in_b, w_in, w_mem_kv, diff_lambda, diff_norm_g,
              w_proj_a, w_proj_b, w_proj_c, w_out, ln_g, ln_b):
    bsz, seq, _ = x.shape
    pos = jnp.arange(seq, dtype=jnp.float32)
    h = _layer_norm(x, ln_in_g, ln_in_b)
    for l in range(DEPTH):
        lam_init = 0.8 - 0.6 * math.exp(-0.3 * l)
        proj = h @ w_in[l]
        (a_q, a_k, a_v, a_gate, i_q, i_k, i_w, b_q, b_k, b_v, b_gate,
         c_q, c_gate, m_gate) = _split_cols(proj, IN_SPLITS)

        a_q = _partial_rope(a_q.reshape(bsz, seq, A_HEADS, A_HEAD_DIM), pos)
        a_k = _partial_rope(a_k[:, :, None, :], pos)[:, :, 0]
        i_q = _partial_rope(i_q.reshape(bsz, seq, IDX_HEADS, IDX_DIM), pos)
        i_k = _partial_rope(i_k[:, :, None, :], pos)[:, :, 0]
        i_w = i_w * ((IDX_HEADS * IDX_DIM) ** -0.5)
        o_a = _dsa_attention(a_q, a_k, a_v, i_q, i_k, i_w).reshape(bsz, seq, A_WIDTH)
        y_a = (o_a * jax.nn.silu(a_gate)) @ w_proj_a[l]

        b_q = _partial_rope(b_q.reshape(bsz, seq, 2 * B_HEADS, B_HEAD_DIM), pos)
        b_k = _partial_rope(b_k.reshape(bsz, seq, 2 * B_HEADS, B_HEAD_DIM), pos)
        b_q = b_q.reshape(bsz, seq, B_HEADS, 2, B_HEAD_DIM)
        b_k = b_k.reshape(bsz, seq, B_HEADS, 2, B_HEAD_DIM)
        b_v = b_v.reshape(bsz, seq, B_HEADS, 2 * B_HEAD_DIM)
        dl = diff_lambda[l].astype(jnp.float32)
        lam = jnp.exp(jnp.sum(dl[0] * dl[1])) - jnp.exp(jnp.sum(dl[2] * dl[3])) + lam_init
        o_b = _diff_attention(b_q, b_k, b_v, lam).astype(jnp.float32)
        o_b = (o_b * lax.rsqrt(jnp.mean(o_b * o_b, -1, keepdims=True) + LN_EPS)
               * diff_norm_g[l].astype(jnp.float32) * (1.0 - lam_init)).astype(h.dtype)
        y_b = (o_b.reshape(bsz, seq, B_WIDTH) * jax.nn.silu(b_gate)) @ w_proj_b[l]

        mk, mv = jnp.split(mem @ w_mem_kv[l], 2, axis=-1)
        mk = mk.reshape(bsz, MEM_LEN, C_HEADS, C_HEAD_DIM)
        mv = mv.reshape(bsz, MEM_LEN, C_HEADS, C_HEAD_DIM)
        o_c = _memory_attention(c_q.reshape(bsz, seq, C_HEADS, C_HEAD_DIM), mk, mv)
        y_c = (o_c.reshape(bsz, seq, C_WIDTH) * jax.nn.silu(c_gate)) @ w_proj_c[l]

        g = jax.nn.sigmoid(m_gate.reshape(bsz, seq, N_BRANCH, D_MODEL))
        merged = g[:, :, 0] * y_a + g[:, :, 1] * y_b + g[:, :, 2] * y_c
        h = _layer_norm(DEEPNORM_ALPHA * h + merged @ w_out[l], ln_g[l], ln_b[l])
    return h
```

```python
import math
from contextlib import ExitStack

import numpy as np
import concourse.bass as bass
import concourse.mybir as mybir
from concourse.bass_utils import run_bass_kernel_spmd

F32 = mybir.dt.float32
BF16 = mybir.dt.bfloat16
AF = mybir.ActivationFunctionType
ALU = mybir.AluOpType
AX = mybir.AxisListType

D = 1024
O_AQ, O_AK, O_AV, O_AG, O_IQ, O_IK, O_IW = 0, 512, 576, 640, 1152, 1408, 1440
O_BQ, O_BK, O_BV, O_BG, O_CQ, O_CG, O_MG = 1448, 1960, 2472, 2984, 3496, 4008, 4520
INW = 7592
LN_EPS = 1e-5
ALPHA = 2.0 ** 0.25
LAM_INIT = 0.8 - 0.6 * math.exp(0.0)
NIT = 18
NEG_BIG = -1.0e30


class R:
    __slots__ = ("w", "rs", "dsem", "dcnt", "excl")

    def __init__(self, excl=False):
        self.excl = excl
        self.w = None
        self.rs = {}
        self.dsem = None
        self.dcnt = 0


class Sch:
    def __init__(self, nc, es):
        self.nc = nc
        self.es = es
        self.eng = {"pe": nc.tensor, "act": nc.scalar, "dve": nc.vector, "pool": nc.gpsimd, "sp": nc.sync}
        self.sems = {}
        self.cnt = {}
        for n in ("pe", "act", "dve", "pool"):
            self.sems[n] = es.enter_context(nc.semaphore("c_" + n))
            self.cnt[n] = 0
        self.waited = {n: {} for n in self.eng}
        self.nd = 0

    def _wait(self, e, key, val):
        if e == "pe" and key == "pe":
            return
        if self.waited[e].get(key, 0) >= val:
            return
        self.waited[e][key] = val
        self.eng[e].wait_ge(self.sems[key], val)

    def _deps(self, e, reads, writes):
        for r in reads:
            if r.w is not None:
                self._wait(e, r.w[0], r.w[1])
            if r.excl:
                for k, v in r.rs.items():
                    if k != e:
                        self._wait(e, k, v)
        for w in writes:
            if w.w is not None:
                self._wait(e, w.w[0], w.w[1])
            for k, v in w.rs.items():
                self._wait(e, k, v)

    def op(self, e, fn, reads=(), writes=(), sig=True):
        self._deps(e, reads, writes)
        inst = fn(self.eng[e])
        if sig:
            self.cnt[e] += 1
            inst.then_inc(self.sems[e], 1)
            dep = (e, self.cnt[e])
        else:
            dep = (e, self.cnt[e] + 1)
        for r in reads:
            if r.rs.get(dep[0], 0) < dep[1]:
                r.rs[dep[0]] = dep[1]
        for w in writes:
            w.w = dep
            w.rs = {}

    def dma(self, q, pairs, reads=(), writes=()):
        self._deps(q, reads, writes)
        res = writes[0] if writes else reads[0]
        if res.dsem is None:
            self.nd += 1
            key = "d%d" % self.nd
            self.sems[key] = self.es.enter_context(self.nc.semaphore(key))
            res.dsem = key
        for (o, i) in pairs:
            res.dcnt += 16
            self.eng[q].dma_start(out=o, in_=i).then_inc(self.sems[res.dsem], 16)
        dep = (res.dsem, res.dcnt)
        for r in reads:
            if r.rs.get(dep[0], 0) < dep[1]:
                r.rs[dep[0]] = dep[1]
        for w in writes:
            w.w = dep
            w.rs = {}

    def barrier(self, extra=()):
        for e in self.eng:
            for r in extra:
                for k, v in r.rs.items():
                    self._wait(e, k, v)
            for f in ("pe", "act", "dve", "pool"):
                if f != e:
                    self._wait(e, f, self.cnt[f])


class _Stop(Exception):
    pass


def build(S, NB, stop=None):
    try:
        return _build(S, NB, stop)
    except _Stop as e:
        return e.args[0]


def _build(S, NB, stop=None):
    NT = S // 128
    NG = S // 512
    NSEL = min(256, S // 4)
    nc = bass.Bass("TRN2", target_bir_lowering=False)

    def din(name, shape):
        return nc.dram_tensor(name, shape, F32, kind="ExternalInput").ap()

    x_d = din("x", [NB, S, D])
    mem_d = din("mem", [NB, 256, D])
    win_d = din("w_in", [D, INW])
    wkv_d = din("w_mem_kv", [D, 1024])
    wp_d = [din("w_proj_a", [512, D]), din("w_proj_b", [512, D]), din("w_proj_c", [512, D])]
    wo_d = din("w_out", [D, D])
    gfm_d = din("g_fm", [128, 8])
    bfm_d = din("b_fm", [128, 8])
    gin_d = din("g_in_bc", [128, D])
    bin_d = din("b_in_bc", [128, D])
    lg_d = din("ln_g_bc", [128, D])
    lb_d = din("ln_b_bc", [128, D])
    dl_d = din("dlam_bc", [128, 256])
    dng_d = din("dng", [128, 1])
    c64_d = din("cos64", [128, S])
    s64_d = din("sp64", [128, S])
    c32_d = din("cos32", [128, S])
    s32_d = din("sp32", [128, S])
    id_d = din("ident", [128, 128])
    p64_d = din("perm64", [128, 128])
    p32_d = din("perm32", [128, 128])
    out_d = nc.dram_tensor("out", [NB, S, D], F32, kind="ExternalOutput").ap()

    win_v = win_d.rearrange("(k p) c -> p k c", p=128)
    wkv_v = wkv_d.rearrange("(k p) c -> p k c", p=128)
    wp_v = [w.rearrange("(k p) c -> p k c", p=128) for w in wp_d]
    wo_v = wo_d.rearrange("(k p) c -> p k c", p=128)

    with ExitStack() as es:
        K = Sch(nc, es)
        cnt_alloc = [0]

        def sb(shape, dt, stack=None):
            cnt_alloc[0] += 1
            return (stack or es).enter_context(nc.sbuf_tensor("t%d" % cnt_alloc[0], shape, dt))

        PS = [es.enter_context(nc.psum_tensor("ps%d" % i, [128, 512], F32)) for i in range(8)]
        PR = [R(excl=True) for _ in range(8)]
        free = list(range(8))

        def getps():
            i = free.pop(0)
            free.append(i)
            return i

        def holdps():
            return free.pop(0)

        def relps(i):
            free.append(i)

        hT = sb([128, 8, S], BF16)
        hR = [[R() for _ in range(8)] for _ in range(NG)]
        akT2 = sb([128, S], BF16)
        akR = [R() for _ in range(NG)]
        ikT3 = sb([128, S], BF16)
        ikR = [R() for _ in range(NG)]
        bkT = sb([128, 4, S], BF16)
        bkR = [[R() for _ in range(4)] for _ in range(NG)]
        avE = sb([128, NT, 128], BF16)
        avO = sb([128, NT, 128], BF16)
        avR = [R() for _ in range(NT)]
        bv = sb([128, NT, 512], BF16)
        bvR = [R() for _ in range(NT)]
        mkT = sb([128, 4, 256], BF16)
        mkR = R()
        mvt = sb([128, 2, 512], BF16)
        mvR = R()
        stats = sb([128, NB * NT, 2], F32)
        statR = [R() for _ in range(NB * NT)]
        xbuf = [(sb([128, D], F32), R()) for _ in range(2)]
        wbuf = [(sb([128, 4096], BF16), R()) for _ in range(3)]
        tabs = {n: (sb([128, 512], F32), R()) for n in ("c64", "s64", "c32", "s32")}
        tab_d = {"c64": c64_d, "s64": s64_d, "c32": c32_d, "s32": s32_d}
        ag = sb([128, D], F32)
        ab = sb([128, D], F32)
        lgb = sb([128, D], F32)
        lbb = sb([128, D], F32)
        cR = R()
        identf = sb([128, 128], F32)
        identb = sb([128, 128], BF16)
        perm64 = sb([128, 128], BF16)
        perm32 = sb([128, 128], BF16)
        I4 = sb([128, 512], BF16)
        onesb = sb([128, 128], BF16)
        gfm = sb([128, 8], F32)
        bfm = sb([128, 8], F32)
        small = sb([128, 16], F32)
        dl = sb([128, 256], F32)
        og = [sb([128, 4, 512], BF16) for _ in range(3)]
        ogR = [[R() for _ in range(4)] for _ in range(3)]
        lnst = sb([128, 16], F32)
        lnR = R()

        K.dma("sp", [(identf[:], id_d[:, :])], writes=[cR])
        cR2 = R()
        K.dma("pool", [(identb[:], id_d[:, :]), (perm64[:], p64_d[:, :]), (perm32[:], p32_d[:, :])]
              + [(I4[:, i * 128:(i + 1) * 128], id_d[:, :]) for i in range(4)], writes=[cR2])
        cR3 = R()
        K.dma("sp", [(gfm[:], gfm_d[:, :]), (bfm[:], bfm_d[:, :]), (ag[:], gin_d[:, :]), (ab[:], bin_d[:, :]),
                     (lgb[:], lg_d[:, :]), (lbb[:], lb_d[:, :]), (dl[:], dl_d[:, :]), (small[:, 3:4], dng_d[:, :])],
              writes=[cR3])
        smR = R()
        K.op("dve", lambda e: e.memset(small[:, 0:1], LN_EPS), writes=[smR])
        K.op("dve", lambda e: e.memset(small[:, 1:2], -1.0e29), writes=[smR])
        K.op("dve", lambda e: e.memset(onesb[:], 1.0), writes=[cR])
        K.op("pool", lambda e: e.memset(avE[:], 1.0), writes=avR)
        K.op("pool", lambda e: e.memset(avO[:], 1.0), writes=avR)
        K.op("dve", lambda e: e.tensor_scalar(out=ag[:], in0=ag[:], scalar1=ALPHA, scalar2=None, op0=ALU.mult),
             reads=[cR3], writes=[cR3])
        K.op("dve", lambda e: e.tensor_scalar(out=ab[:], in0=ab[:], scalar1=ALPHA, scalar2=None, op0=ALU.mult),
             reads=[cR3], writes=[cR3])
        K.op("dve", lambda e: e.tensor_scalar(out=small[:, 3:4], in0=small[:, 3:4], scalar1=1.0 - LAM_INIT,
                                              scalar2=None, op0=ALU.mult), reads=[cR3, smR], writes=[smR])
        K.op("dve", lambda e: e.tensor_tensor(out=dl[:, 0:64], in0=dl[:, 0:64], in1=dl[:, 64:128], op=ALU.mult),
             reads=[cR3], writes=[cR3])
        K.op("dve", lambda e: e.tensor_tensor(out=dl[:, 128:192], in0=dl[:, 128:192], in1=dl[:, 192:256], op=ALU.mult),
             reads=[cR3], writes=[cR3])
        K.op("dve", lambda e: e.reduce_sum(out=small[:, 4:5], in_=dl[:, 0:64], axis=AX.X), reads=[cR3], writes=[smR])
        K.op("dve", lambda e: e.reduce_sum(out=small[:, 5:6], in_=dl[:, 128:192], axis=AX.X), reads=[cR3], writes=[smR])
        K.op("act", lambda e: e.activation(out=small[:, 4:6], in_=small[:, 4:6], func=AF.Exp), reads=[smR], writes=[smR])
        K.op("dve", lambda e: e.tensor_tensor(out=small[:, 2:3], in0=small[:, 5:6], in1=small[:, 4:5], op=ALU.subtract),
             reads=[smR], writes=[smR])
        K.op("dve", lambda e: e.tensor_scalar(out=small[:, 2:3], in0=small[:, 2:3], scalar1=-LAM_INIT, scalar2=None,
                                              op0=ALU.add), reads=[smR], writes=[smR])
        constR = [cR, cR2, cR3, smR]

        def ck(tag):
            if stop == tag:
                K.barrier()
                raise _Stop(nc)
        ck('setup')

        def w8(i):
            return wbuf[i][0][:, :].rearrange("p (k c) -> p k c", k=8)

        def w4(i):
            return wbuf[i][0][:, 0:2048].rearrange("p (k c) -> p k c", k=4)

        specs = []
        for b in range(NB):
            specs.append(("K1", 8, [(win_v, O_AK, 64, 0), (win_v, O_AK, 64, 64), (win_v, O_IK, 32, 128),
                                    (win_v, O_IK, 32, 160), (win_v, O_IK, 32, 192), (win_v, O_AV, 64, 224)]))
            specs.append(("K2", 8, [(win_v, O_BK, 512, 0)]))
            specs.append(("K3", 8, [(win_v, O_BV, 512, 0)]))
            specs.append(("M1", 8, [(wkv_v, 0, 512, 0)]))
            specs.append(("M2", 8, [(wkv_v, 512, 512, 0)]))
            for G in range(NG):
                specs.append(("A1", 8, [(win_v, O_AQ, 512, 0)]))
                specs.append(("A2", 8, [(win_v, O_IQ, 256, 0), (win_v, O_IW, 8, 256)]))
                specs.append(("A3", 8, [(win_v, O_AG, 512, 0)]))
                specs.append(("B1", 8, [(win_v, O_BQ, 512, 0)]))
                specs.append(("B2", 8, [(win_v, O_BG, 512, 0)]))
                specs.append(("C1", 8, [(win_v, O_CQ, 512, 0)]))
                specs.append(("C2", 8, [(win_v, O_CG, 512, 0)]))
                for bi in range(3):
                    for hf in range(2):
                        specs.append(("Y%d%d" % (bi, hf), 4, [(wp_v[bi], hf * 512, 512, 0)]))
                        specs.append(("G%d%d" % (bi, hf), 8, [(win_v, O_MG + bi * 1024 + hf * 512, 512, 0)]))
                specs.append(("O0", 8, [(wo_v, 0, 512, 0)]))
                specs.append(("O1", 8, [(wo_v, 512, 512, 0)]))
        wst = {"issued": 0, "got": 0}
        NWB = len(wbuf)
        PREF = 1

        def w_issue():
            i = wst["issued"]
            tag, nk, pieces = specs[i]
            bi = i % NWB
            view = w8(bi) if nk == 8 else w4(bi)
            pairs = [(view[:, :, off:off + n], src[:, :, c0:c0 + n]) for (src, c0, n, off) in pieces]
            K.dma("pool", pairs, writes=[wbuf[bi][1]])
            wst["issued"] += 1

        def wget(tag):
            i = wst["got"]
            assert specs[i][0] == tag, (specs[i][0], tag)
            while wst["issued"] < min(len(specs), i + 1 + PREF):
                w_issue()
            wst["got"] += 1
            bi = i % NWB
            return (w8(bi) if specs[i][1] == 8 else w4(bi)), wbuf[bi][1]

        def proj_fm(wt, wR, col0, M, tok0, ntok, bank):
            G = tok0 // 512
            for kc in range(8):
                K.op("pe", lambda e, kc=kc: e.matmul(PS[bank][0:M, 0:ntok], lhsT=wt[:, kc, col0:col0 + M],
                                                     rhs=hT[:, kc, tok0:tok0 + ntok], start=(kc == 0), stop=(kc == 7)),
                     reads=[wR, hR[G][kc]], writes=[PR[bank]], sig=(kc == 7))

        def load_tables(G):
            for n in ("c64", "s64", "c32", "s32"):
                t, tR = tabs[n]
                K.dma("sp", [(t[:], tab_d[n][:, G * 512:(G + 1) * 512])], writes=[tR])

        rp = {"i": 0}

        def rope_evac(bank, M, hd, rtmp, outs):
            ct, cR_ = tabs["c64" if hd == 64 else "c32"]
            st_, sR_ = tabs["s64" if hd == 64 else "s32"]
            perm = perm64 if hd == 64 else perm32
            (c, cRr), (u, uRr) = rtmp[rp["i"] % len(rtmp)]
            rp["i"] += 1
            K.op("dve", lambda e: e.tensor_tensor(out=c[0:M, :], in0=PS[bank][0:M, 0:512], in1=ct[0:M, :], op=ALU.mult),
                 reads=[PR[bank], cR_], writes=[cRr])
            K.op("dve", lambda e: e.tensor_tensor(out=u[0:M, :], in0=PS[bank][0:M, 0:512], in1=st_[0:M, :], op=ALU.mult),
                 reads=[PR[bank], sR_], writes=[uRr])
            b2 = getps()
            K.op("pe", lambda e: e.matmul(PS[b2][0:M, 0:512], lhsT=identb[0:M, 0:M], rhs=c[0:M, :], start=True, stop=False),
                 reads=[cRr, cR2], writes=[PR[b2]], sig=False)
            K.op("pe", lambda e: e.matmul(PS[b2][0:M, 0:512], lhsT=perm[0:M, 0:M], rhs=u[0:M, :], start=False, stop=True),
                 reads=[uRr, cR2], writes=[PR[b2]], sig=True)
            for (dst, dR, r0, r1) in outs:
                K.op("act", lambda e, dst=dst, r0=r0, r1=r1: e.activation(out=dst, in_=PS[b2][r0:r1, 0:512], func=AF.Copy),
                     reads=[PR[b2]], writes=[dR])

        def ln_stats(src, srcR, mean_rstd_out, outR):
            K.op("dve", lambda e: e.bn_stats(out=lnst[:, 0:6], in_=src[:, 0:512]), reads=[srcR], writes=[lnR])
            K.op("dve", lambda e: e.bn_stats(out=lnst[:, 6:12], in_=src[:, 512:1024]), reads=[srcR], writes=[lnR])
            K.op("dve", lambda e: e.bn_aggr(out=lnst[:, 12:14], in_=lnst[:, 0:12]), reads=[lnR], writes=[lnR])
            K.op("act", lambda e: e.activation(out=lnst[:, 14:15], in_=lnst[:, 13:14], func=AF.Sqrt, bias=small[:, 0:1],
                                               scale=1.0), reads=[lnR, smR], writes=[lnR])
            K.op("dve", lambda e: e.reciprocal(out=mean_rstd_out[:, 1:2], in_=lnst[:, 14:15]), reads=[lnR], writes=[outR])
            K.op("dve", lambda e: e.tensor_scalar(out=mean_rstd_out[:, 0:1], in0=lnst[:, 12:13],
                                                  scalar1=mean_rstd_out[:, 1:2], scalar2=-1.0, op0=ALU.mult, op1=ALU.mult),
                 reads=[lnR, outR], writes=[outR])

        xi = {"i": 0}

        def next_xbuf():
            r = xbuf[xi["i"] % 2]
            xi["i"] += 1
            return r

        for b in range(NB):
            for tt in range(NT):
                xb, xR = next_xbuf()
                sidx = b * NT + tt
                K.dma("sp", [(xb[:], x_d[b, tt * 128:(tt + 1) * 128, :])], writes=[xR])
                ln_stats(xb, xR, stats[:, sidx, :], statR[sidx])
                K.op("act", lambda e, xb=xb, sidx=sidx: e.activation(out=xb[:], in_=xb[:], func=AF.Identity,
                                                                      scale=stats[:, sidx, 1:2], bias=stats[:, sidx, 0:1]),
                     reads=[xR, statR[sidx]], writes=[xR])
                for half in range(2):
                    bk = getps()
                    for q in range(4):
                        kc = half * 4 + q
                        K.op("pe", lambda e, xb=xb, kc=kc, q=q, bk=bk: e.transpose(
                            out=PS[bk][:, q * 128:(q + 1) * 128], in_=xb[:, kc * 128:(kc + 1) * 128], identity=identf[:]),
                             reads=[xR, cR], writes=[PR[bk]], sig=(q == 3))
                    for q in range(4):
                        kc = half * 4 + q
                        dst = hT[:, kc, tt * 128:(tt + 1) * 128]
                        if kc % 2 == 0:
                            K.op("act", lambda e, dst=dst, bk=bk, q=q, kc=kc: e.activation(
                                out=dst, in_=PS[bk][:, q * 128:(q + 1) * 128], func=AF.Identity,
                                scale=gfm[:, kc:kc + 1], bias=bfm[:, kc:kc + 1]),
                                 reads=[PR[bk], cR3], writes=[hR[tt // 4][kc]])
                        else:
                            K.op("dve", lambda e, dst=dst, bk=bk, q=q, kc=kc: e.tensor_scalar(
                                out=dst, in0=PS[bk][:, q * 128:(q + 1) * 128], scalar1=gfm[:, kc:kc + 1],
                                scalar2=bfm[:, kc:kc + 1], op0=ALU.mult, op1=ALU.add),
                                 reads=[PR[bk], cR3], writes=[hR[tt // 4][kc]])

            ck('p1')
            with ExitStack() as kvs:
                memT = sb([128, 8, 256], BF16, kvs)
                memR = R()
                rtmp = [((sb([128, 512], BF16, kvs), R()), (sb([128, 512], BF16, kvs), R())) for _ in range(2)]
                for mt in range(2):
                    xb, xR = next_xbuf()
                    K.dma("sp", [(xb[:], mem_d[b, mt * 128:(mt + 1) * 128, :])], writes=[xR])
                    for half in range(2):
                        bk = getps()
                        for q in range(4):
                            kc = half * 4 + q
                            K.op("pe", lambda e, xb=xb, kc=kc, q=q, bk=bk: e.transpose(
                                out=PS[bk][:, q * 128:(q + 1) * 128], in_=xb[:, kc * 128:(kc + 1) * 128], identity=identf[:]),
                                 reads=[xR, cR], writes=[PR[bk]], sig=(q == 3))
                        K.op("act", lambda e, bk=bk, half=half, mt=mt: e.activation(
                            out=memT[:, half * 4:half * 4 + 4, mt * 128:(mt + 1) * 128],
                            in_=PS[bk][:, :].rearrange("p (k c) -> p k c", k=4), func=AF.Copy),
                             reads=[PR[bk]], writes=[memR])
                wt, wR = wget("K1")
                for G in range(NG):
                    load_tables(G)
                    bk = getps()
                    proj_fm(wt, wR, 0, 128, G * 512, 512, bk)
                    rope_evac(bk, 128, 64, rtmp, [(akT2[:, G * 512:(G + 1) * 512], akR[G], 0, 128)])
                    bk = getps()
                    proj_fm(wt, wR, 128, 96, G * 512, 512, bk)
                    rope_evac(bk, 96, 32, rtmp, [(ikT3[0:96, G * 512:(G + 1) * 512], ikR[G], 0, 96)])
                    for t in range(4):
                        tt = 4 * G + t
                        bk = getps()
                        for kc in range(8):
                            K.op("pe", lambda e, kc=kc, tt=tt, bk=bk: e.matmul(
                                PS[bk][:, 0:64], lhsT=hT[:, kc, tt * 128:(tt + 1) * 128], rhs=wt[:, kc, 224:288],
                                start=(kc == 0), stop=(kc == 7)), reads=[wR, hR[G][kc]], writes=[PR[bk]], sig=(kc == 7))
                        K.op("act", lambda e, tt=tt, bk=bk: e.activation(out=avE[:, tt, 0:64], in_=PS[bk][:, 0:64],
                                                                         func=AF.Copy), reads=[PR[bk]], writes=[avR[tt]])
                        K.op("dve", lambda e, tt=tt, bk=bk: e.tensor_copy(out=avO[:, tt, 64:128], in_=PS[bk][:, 0:64]),
                             reads=[PR[bk]], writes=[avR[tt]])
                wt, wR = wget("K2")
                for G in range(NG):
                    load_tables(G)
                    for ch in range(4):
                        bk = getps()
                        proj_fm(wt, wR, ch * 128, 128, G * 512, 512, bk)
                        rope_evac(bk, 128, 64, rtmp, [(bkT[:, ch, G * 512:(G + 1) * 512], bkR[G][ch], 0, 128)])
                wt, wR = wget("K3")
                for tt in range(NT):
                    bk = getps()
                    for kc in range(8):
                        K.op("pe", lambda e, kc=kc, tt=tt, bk=bk: e.matmul(
                            PS[bk][:, 0:512], lhsT=hT[:, kc, tt * 128:(tt + 1) * 128], rhs=wt[:, kc, 0:512],
                            start=(kc == 0), stop=(kc == 7)), reads=[wR, hR[tt // 4][kc]], writes=[PR[bk]], sig=(kc == 7))
                    if tt % 2 == 0:
                        K.op("act", lambda e, tt=tt, bk=bk: e.activation(out=bv[:, tt, :], in_=PS[bk][:, 0:512], func=AF.Copy),
                             reads=[PR[bk]], writes=[bvR[tt]])
                    else:
                        K.op("dve", lambda e, tt=tt, bk=bk: e.tensor_copy(out=bv[:, tt, :], in_=PS[bk][:, 0:512]),
                             reads=[PR[bk]], writes=[bvR[tt]])
                wt, wR = wget("M1")
                for hc in range(4):
                    bk = getps()
                    for kc in range(8):
                        K.op("pe", lambda e, kc=kc, hc=hc, bk=bk: e.matmul(
                            PS[bk][:, 0:256], lhsT=wt[:, kc, hc * 128:(hc + 1) * 128], rhs=memT[:, kc, :],
                            start=(kc == 0), stop=(kc == 7)), reads=[wR, memR], writes=[PR[bk]], sig=(kc == 7))
                    K.op("act", lambda e, hc=hc, bk=bk: e.activation(out=mkT[:, hc, :], in_=PS[bk][:, 0:256], func=AF.Copy),
                         reads=[PR[bk]], writes=[mkR])
                wt, wR = wget("M2")
                for mc in range(2):
                    bk = getps()
                    for kc in range(8):
                        K.op("pe", lambda e, kc=kc, mc=mc, bk=bk: e.matmul(
                            PS[bk][:, 0:512], lhsT=memT[:, kc, mc * 128:(mc + 1) * 128], rhs=wt[:, kc, 0:512],
                            start=(kc == 0), stop=(kc == 7)), reads=[wR, memR], writes=[PR[bk]], sig=(kc == 7))
                    K.op("dve", lambda e, mc=mc, bk=bk: e.tensor_copy(out=mvt[:, mc, :], in_=PS[bk][:, 0:512]),
                         reads=[PR[bk]], writes=[mvR])
                K.barrier()

            for G in range(NG):
                tok0 = G * 512
                with ExitStack() as ats:
                    rtmp = [((sb([128, 512], BF16, ats), R()), (sb([128, 512], BF16, ats), R())) for _ in range(2)]
                    Sti = sb([128, S], F32, ats)
                    SR = R()
                    MB = [(sb([128, S], BF16, ats), R()) for _ in range(2)]
                    qT = sb([128, 4, 512], BF16, ats)
                    qR = [R() for _ in range(4)]
                    iqT = sb([128, 3, 512], BF16, ats)
                    iqR = [R() for _ in range(3)]
                    gateT = sb([128, 4, 512], BF16, ats)
                    gR = [R() for _ in range(4)]
                    on = sb([128, 4, 512], BF16, ats)
                    onR = [R() for _ in range(4)]
                    pTA = [(sb([128, 512], BF16, ats), R()) for _ in range(4)]
                    ft = [(sb([128, 512], F32, ats), R()) for _ in range(3)]
                    sqb = (sb([128, 512], BF16, ats), R())
                    wab = sb([128, 4, 8], F32, ats)
                    wsg = sb([128, 4, 8], F32, ats)
                    wR_ = [R() for _ in range(4)]
                    bis = sb([128, 4], F32, ats)
                    bisR = R()
                    thr = sb([128, 4], F32, ats)
                    thrR = [R() for _ in range(4)]
                    pti = {"i": 0}

                    def next_pt():
                        r = pTA[pti["i"] % 4]
                        pti["i"] += 1
                        return r

                    ck('kv')
                    load_tables(G)
                    wt, wR = wget("A1")
                    for ch in range(4):
                        bk = getps()
                        proj_fm(wt, wR, ch * 128, 128, tok0, 512, bk)
                        rope_evac(bk, 128, 64, rtmp, [(qT[:, ch, :], qR[ch], 0, 128)])
                    wt, wR = wget("A2")
                    for ci, (c0, M) in enumerate(((0, 96), (96, 96), (192, 64))):
                        bk = getps()
                        proj_fm(wt, wR, c0, M, tok0, 512, bk)
                        rope_evac(bk, M, 32, rtmp, [(iqT[0:M, ci, :], iqR[ci], 0, M)])
                    for jj in range(4):
                        j = 4 * G + jj
                        bk = getps()
                        for kc in range(8):
                            K.op("pe", lambda e, kc=kc, j=j, bk=bk: e.matmul(
                                PS[bk][:, 0:8], lhsT=hT[:, kc, j * 128:(j + 1) * 128], rhs=wt[:, kc, 256:264],
                                start=(kc == 0), stop=(kc == 7)), reads=[wR, hR[G][kc]], writes=[PR[bk]], sig=(kc == 7))
                        K.op("dve", lambda e, jj=jj, bk=bk: e.tensor_scalar(
                            out=wsg[:, jj, :], in0=PS[bk][:, 0:8], scalar1=0.0, scalar2=2.0,
                            op0=ALU.is_ge, op1=ALU.mult), reads=[PR[bk]], writes=[wR_[jj]])
                        K.op("dve", lambda e, jj=jj: e.tensor_scalar(
                            out=wsg[:, jj, :], in0=wsg[:, jj, :], scalar1=-1.0, scalar2=None, op0=ALU.add),
                             reads=[wR_[jj]], writes=[wR_[jj]])
                        K.op("dve", lambda e, jj=jj, bk=bk: e.scalar_tensor_tensor(
                            out=wab[:, jj, :], in0=PS[bk][:, 0:8], scalar=1.0 / 16.0, in1=wsg[:, jj, :],
                            op0=ALU.mult, op1=ALU.mult), reads=[PR[bk], wR_[jj]], writes=[wR_[jj]])
                    ck('a2')
                    def emit_idx(jj):
                        j = 4 * G + jj
                        Wk = (j + 1) * 128
                        mb, mbR = MB[jj % 2]
                        for kt in range((Wk + 511) // 512):
                            k0 = kt * 512
                            n = min(512, Wk - k0)
                            for h in range(8):
                                ci, pos = h // 3, h % 3
                                bk = getps()
                                K.op("pe", lambda e, bk=bk, ci=ci, pos=pos, jj=jj, k0=k0, n=n: e.matmul(
                                    PS[bk][:, 0:n], lhsT=iqT[pos * 32:(pos + 1) * 32, ci, jj * 128:(jj + 1) * 128],
                                    rhs=ikT3[pos * 32:(pos + 1) * 32, k0:k0 + n], start=True, stop=True),
                                     reads=[iqR[ci], ikR[kt]], writes=[PR[bk]])
                                K.op("act", lambda e, bk=bk, n=n, jj=jj, h=h: e.activation(
                                    out=PS[bk][:, 0:n], in_=PS[bk][:, 0:n], func=AF.Relu, scale=wab[:, jj, h:h + 1]),
                                     reads=[PR[bk], wR_[jj]], writes=[PR[bk]])
                                if h == 0:
                                    K.op("dve", lambda e, bk=bk, n=n, jj=jj, k0=k0: e.tensor_scalar(
                                        out=Sti[:, k0:k0 + n], in0=PS[bk][:, 0:n], scalar1=wsg[:, jj, 0:1], scalar2=None,
                                        op0=ALU.mult), reads=[PR[bk], wR_[jj]], writes=[SR])
                                else:
                                    K.op("dve", lambda e, bk=bk, n=n, jj=jj, k0=k0, h=h: e.scalar_tensor_tensor(
                                        out=Sti[:, k0:k0 + n], in0=PS[bk][:, 0:n], scalar=wsg[:, jj, h:h + 1],
                                        in1=Sti[:, k0:k0 + n], op0=ALU.mult, op1=ALU.add),
                                         reads=[PR[bk], wR_[jj], SR], writes=[SR])
                        K.op("dve", lambda e, Wk=Wk: e.memset(Sti[0:64, Wk - 64:Wk], NEG_BIG), reads=[SR], writes=[SR])
                        if Wk <= NSEL:
                            thr_ap = small[:, 1:2]
                            thr_R = smR
                        else:
                            K.op("dve", lambda e: e.memset(bis[:, 0:1], 0.0), writes=[bisR])
                            hstep = 32.0
                            for it in range(1, NIT + 1):
                                K.op("dve", lambda e, Wk=Wk, mb=mb: e.tensor_scalar(
                                    out=mb[:, 0:Wk], in0=Sti[:, 0:Wk], scalar1=bis[:, 0:1], scalar2=0.0,
                                    op0=ALU.is_ge, op1=ALU.add, accum_out=bis[:, 1:2]),
                                     reads=[SR, bisR], writes=[mbR, bisR])
                                K.op("dve", lambda e, hstep=hstep: e.tensor_scalar(
                                    out=bis[:, 2:3], in0=bis[:, 1:2], scalar1=float(NSEL) - 0.5, scalar2=hstep,
                                    op0=ALU.is_ge, op1=ALU.mult), reads=[bisR], writes=[bisR])
                                cc = hstep / 2.0 if it < NIT else hstep
                                dst = bis[:, 0:1] if it < NIT else thr[:, jj:jj + 1]
                                dR_ = bisR if it < NIT else thrR[jj]
                                K.op("dve", lambda e, cc=cc, dst=dst: e.scalar_tensor_tensor(
                                    out=dst, in0=bis[:, 0:1], scalar=-cc, in1=bis[:, 2:3], op0=ALU.add, op1=ALU.add),
                                     reads=[bisR], writes=[dR_, bisR])
                                hstep /= 2.0
                            thr_ap = thr[:, jj:jj + 1]
                            thr_R = thrR[jj]
                        K.op("dve", lambda e, Wk=Wk, mb=mb, thr_ap=thr_ap: e.tensor_scalar(
                            out=mb[:, 0:Wk], in0=Sti[:, 0:Wk], scalar1=thr_ap, scalar2=-30000.0,
                            op0=ALU.is_lt, op1=ALU.mult), reads=[SR, thr_R], writes=[mbR])

                    def emit_attn(jj):
                        j = 4 * G + jj
                        mb, mbR = MB[jj % 2]
                        oa = [holdps(), holdps()]
                        for c in range(j + 1):
                            for par in range(2):
                                avt = avE if par == 0 else avO
                                lg = getps()
                                K.op("pe", lambda e, lg=lg, mb=mb, c=c: e.matmul(
                                    PS[lg][:, 0:512], lhsT=mb[:, c * 128:(c + 1) * 128], rhs=I4[:, :], start=True, stop=False),
                                     reads=[mbR, cR2], writes=[PR[lg]], sig=False)
                                for hh in range(4):
                                    K.op("pe", lambda e, lg=lg, hh=hh, par=par, c=c, jj=jj: e.matmul(
                                        PS[lg][:, hh * 128:(hh + 1) * 128],
                                        lhsT=akT2[par * 64:(par + 1) * 64, c * 128:(c + 1) * 128],
                                        rhs=qT[par * 64:(par + 1) * 64, hh, jj * 128:(jj + 1) * 128],
                                        start=False, stop=(hh == 3)),
                                         reads=[akR[c // 4], qR[hh]], writes=[PR[lg]], sig=(hh == 3))
                                pt, ptR = next_pt()
                                K.op("act", lambda e, lg=lg, pt=pt: e.activation(
                                    out=pt[:, :], in_=PS[lg][:, 0:512], func=AF.Exp, scale=0.125),
                                     reads=[PR[lg]], writes=[ptR])
                                K.op("pe", lambda e, par=par, avt=avt, c=c, pt=pt, j=j: e.matmul(
                                    PS[oa[par]][:, 0:512], lhsT=avt[:, c, :], rhs=pt[:, :], start=(c == 0), stop=(c == j)),
                                     reads=[avR[c], ptR], writes=[PR[oa[par]]])
                        for par in range(2):
                            rec, recR = ft[par]
                            orow = slice(par * 64, par * 64 + 64)
                            drow = slice((1 - par) * 64, (1 - par) * 64 + 64)
                            K.op("act", lambda e, par=par, rec=rec, drow=drow: e.activation(
                                out=rec[drow, :], in_=PS[oa[par]][drow, 0:512], func=AF.Ln), reads=[PR[oa[par]]], writes=[recR])
                            K.op("act", lambda e, rec=rec, drow=drow: e.activation(
                                out=rec[drow, :], in_=rec[drow, :], func=AF.Exp, scale=-1.0), reads=[recR], writes=[recR])
                            K.op("dve", lambda e, par=par, rec=rec, drow=drow, orow=orow, jj=jj: e.tensor_tensor(
                                out=on[orow, :, jj * 128:(jj + 1) * 128],
                                in0=PS[oa[par]][orow, 0:512].rearrange("p (h q) -> p h q", h=4),
                                in1=rec[drow, :].rearrange("p (h q) -> p h q", h=4), op=ALU.mult),
                                 reads=[PR[oa[par]], recR], writes=onR)
                        relps(oa[0])
                        relps(oa[1])

                    emit_idx(0)
                    for jj in range(4):
                        if jj + 1 < 4:
                            emit_idx(jj + 1)
                        emit_attn(jj)
                    ck('attA')
                    wt, wR = wget("A3")
                    for ch in range(4):
                        bk = getps()
                        proj_fm(wt, wR, ch * 128, 128, tok0, 512, bk)
                        K.op("act", lambda e, bk=bk, ch=ch: e.activation(out=gateT[:, ch, :], in_=PS[bk][:, 0:512], func=AF.Silu),
                             reads=[PR[bk]], writes=[gR[ch]])
                        K.op("pool", lambda e, ch=ch: e.tensor_tensor(out=og[0][:, ch, :], in0=on[:, ch, :], in1=gateT[:, ch, :],
                                                                      op=ALU.mult), reads=[onR[ch], gR[ch]] + onR, writes=[ogR[0][ch]])
                    ck('a3')
                    wt, wR = wget("B1")
                    for ch in range(4):
                        bk = getps()
                        proj_fm(wt, wR, ch * 128, 128, tok0, 512, bk)
                        rope_evac(bk, 128, 64, rtmp, [(qT[:, ch, :], qR[ch], 0, 128)])
                    nch = 4 * G + 4
                    for hb in range(4):
                        ob_ = [holdps(), holdps()]
                        db_ = [holdps(), holdps()]
                        for c in range(nch):
                            i = c - 4 * G
                            q0 = 0 if i < 0 else i * 128
                            nq = 512 - q0
                            for m in range(2):
                                lg = getps()
                                K.op("pe", lambda e, lg=lg, m=m, hb=hb, c=c, q0=q0, nq=nq: e.matmul(
                                    PS[lg][:, 0:nq], lhsT=bkT[m * 64:(m + 1) * 64, hb, c * 128:(c + 1) * 128],
                                    rhs=qT[m * 64:(m + 1) * 64, hb, q0:512], start=True, stop=True),
                                     reads=[bkR[c // 4][hb], qR[hb]], writes=[PR[lg]])
                                pt, ptR = next_pt()
                                K.op("act", lambda e, lg=lg, pt=pt, nq=nq: e.activation(
                                    out=pt[:, 0:nq], in_=PS[lg][:, 0:nq], func=AF.Exp, scale=0.125),
                                     reads=[PR[lg]], writes=[ptR])
                                if i >= 0:
                                    K.op("pool", lambda e, pt=pt: e.memset(pt[64:128, 0:64], 0.0), reads=[ptR], writes=[ptR])
                                K.op("pe", lambda e, m=m, hb=hb, c=c, pt=pt, q0=q0, nq=nq: e.matmul(
                                    PS[ob_[m]][:, q0:512], lhsT=bv[:, c, hb * 128:(hb + 1) * 128], rhs=pt[:, 0:nq],
                                    start=(c == 0), stop=(c == nch - 1)), reads=[bvR[c], ptR], writes=[PR[ob_[m]]])
                                K.op("pe", lambda e, m=m, c=c, pt=pt, q0=q0, nq=nq: e.matmul(
                                    PS[db_[m]][:, q0:512], lhsT=onesb[:, :], rhs=pt[:, 0:nq],
                                    start=(c == 0), stop=(c == nch - 1)), reads=[cR, ptR], writes=[PR[db_[m]]])
                        (t0, t0R), (t1, t1R), (t2, t2R) = ft
                        K.op("act", lambda e: e.activation(out=t2[:, :], in_=PS[db_[0]][:, 0:512], func=AF.Ln), reads=[PR[db_[0]]], writes=[t2R])
                        K.op("act", lambda e: e.activation(out=t2[:, :], in_=t2[:, :], func=AF.Exp, scale=-1.0), reads=[t2R], writes=[t2R])
                        K.op("dve", lambda e: e.tensor_tensor(out=t0[:, :], in0=PS[ob_[0]][:, 0:512], in1=t2[:, :], op=ALU.mult),
                             reads=[PR[ob_[0]], t2R], writes=[t0R])
                        K.op("act", lambda e: e.activation(out=t2[:, :], in_=PS[db_[1]][:, 0:512], func=AF.Ln), reads=[PR[db_[1]], t0R],
                             writes=[t2R])
                        K.op("act", lambda e: e.activation(out=t2[:, :], in_=t2[:, :], func=AF.Exp, scale=-1.0), reads=[t2R], writes=[t2R])
                        K.op("dve", lambda e: e.tensor_tensor(out=t1[:, :], in0=PS[ob_[1]][:, 0:512], in1=t2[:, :], op=ALU.mult),
                             reads=[PR[ob_[1]], t2R], writes=[t1R])
                        K.op("dve", lambda e: e.scalar_tensor_tensor(out=t0[:, :], in0=t1[:, :], scalar=small[:, 2:3], in1=t0[:, :],
                                                                     op0=ALU.mult, op1=ALU.add), reads=[t1R, t0R, smR], writes=[t0R])
                        for bnk in ob_ + db_:
                            relps(bnk)
                        sq, sqR = sqb
                        K.op("act", lambda e: e.activation(out=sq[:, :], in_=t0[:, :], func=AF.Square), reads=[t0R], writes=[sqR])
                        ms = getps()
                        K.op("pe", lambda e, ms=ms: e.matmul(PS[ms][:, 0:512], lhsT=onesb[:, :], rhs=sq[:, :], start=True, stop=True),
                             reads=[sqR, cR], writes=[PR[ms]])
                        K.op("act", lambda e, ms=ms: e.activation(out=t1[:, :], in_=PS[ms][:, 0:512], func=AF.Ln,
                                                                  bias=small[:, 0:1], scale=1.0 / 128.0),
                             reads=[PR[ms], smR, t1R], writes=[t1R])
                        K.op("act", lambda e: e.activation(out=t1[:, :], in_=t1[:, :], func=AF.Exp, scale=-0.5), reads=[t1R], writes=[t1R])
                        K.op("dve", lambda e, hb=hb: e.scalar_tensor_tensor(out=on[:, hb, :], in0=t0[:, :], scalar=small[:, 3:4],
                                                                            in1=t1[:, :], op0=ALU.mult, op1=ALU.mult),
                             reads=[t0R, t1R, smR], writes=[onR[hb]])
                    wt, wR = wget("B2")
                    for ch in range(4):
                        bk = getps()
                        proj_fm(wt, wR, ch * 128, 128, tok0, 512, bk)
                        K.op("act", lambda e, bk=bk, ch=ch: e.activation(out=gateT[:, ch, :], in_=PS[bk][:, 0:512], func=AF.Silu),
                             reads=[PR[bk]], writes=[gR[ch]])
                        K.op("pool", lambda e, ch=ch: e.tensor_tensor(out=og[1][:, ch, :], in0=on[:, ch, :], in1=gateT[:, ch, :],
                                                                      op=ALU.mult), reads=[onR[ch], gR[ch]], writes=[ogR[1][ch]])
                    ck('b')
                    wt, wR = wget("C1")
                    for ch in range(4):
                        bk = getps()
                        proj_fm(wt, wR, ch * 128, 128, tok0, 512, bk)
                        K.op("act", lambda e, bk=bk, ch=ch: e.activation(out=qT[:, ch, :], in_=PS[bk][:, 0:512], func=AF.Copy),
                             reads=[PR[bk]], writes=[qR[ch]])
                    for hc in range(4):
                        oc_ = holdps()
                        dc_ = holdps()
                        for mc in range(2):
                            lg = getps()
                            K.op("pe", lambda e, lg=lg, hc=hc, mc=mc: e.matmul(
                                PS[lg][:, 0:512], lhsT=mkT[:, hc, mc * 128:(mc + 1) * 128], rhs=qT[:, hc, :],
                                start=True, stop=True), reads=[mkR, qR[hc]], writes=[PR[lg]])
                            pt, ptR = next_pt()
                            K.op("act", lambda e, lg=lg, pt=pt: e.activation(
                                out=pt[:, :], in_=PS[lg][:, 0:512], func=AF.Exp, scale=128.0 ** -0.5),
                                 reads=[PR[lg]], writes=[ptR])
                            K.op("pe", lambda e, hc=hc, mc=mc, pt=pt, oc_=oc_: e.matmul(
                                PS[oc_][:, 0:512], lhsT=mvt[:, mc, hc * 128:(hc + 1) * 128], rhs=pt[:, :],
                                start=(mc == 0), stop=(mc == 1)), reads=[mvR, ptR], writes=[PR[oc_]])
                            K.op("pe", lambda e, mc=mc, pt=pt, dc_=dc_: e.matmul(
                                PS[dc_][:, 0:512], lhsT=onesb[:, :], rhs=pt[:, :],
                                start=(mc == 0), stop=(mc == 1)), reads=[cR, ptR], writes=[PR[dc_]])
                        t2, t2R = ft[2]
                        K.op("act", lambda e, dc_=dc_: e.activation(out=t2[:, :], in_=PS[dc_][:, 0:512], func=AF.Ln), reads=[PR[dc_]], writes=[t2R])
                        K.op("act", lambda e: e.activation(out=t2[:, :], in_=t2[:, :], func=AF.Exp, scale=-1.0), reads=[t2R], writes=[t2R])
                        K.op("dve", lambda e, oc_=oc_, hc=hc: e.tensor_tensor(out=on[:, hc, :], in0=PS[oc_][:, 0:512], in1=t2[:, :],
                                                                              op=ALU.mult), reads=[PR[oc_], t2R], writes=[onR[hc]])
                        relps(oc_)
                        relps(dc_)
                    wt, wR = wget("C2")
                    for ch in range(4):
                        bk = getps()
                        proj_fm(wt, wR, ch * 128, 128, tok0, 512, bk)
                        K.op("act", lambda e, bk=bk, ch=ch: e.activation(out=gateT[:, ch, :], in_=PS[bk][:, 0:512], func=AF.Silu),
                             reads=[PR[bk]], writes=[gR[ch]])
                        K.op("pool", lambda e, ch=ch: e.tensor_tensor(out=og[2][:, ch, :], in0=on[:, ch, :], in1=gateT[:, ch, :],
                                                                      op=ALU.mult), reads=[onR[ch], gR[ch]], writes=[ogR[2][ch]])
                    K.barrier()
                ck('attn')
                with ExitStack() as ots:
                    merged = sb([128, 8, 512], F32, ots)
                    mR = [R() for _ in range(8)]
                    mbf = sb([128, 8, 512], BF16, ots)
                    mbR_ = [R() for _ in range(8)]
                    gsig = [(sb([128, 512], F32, ots), R()) for _ in range(2)]
                    tmpf = [(sb([128, 512], F32, ots), R()) for _ in range(2)]
                    wk = [(sb([128, D], F32, ots), R()) for _ in range(2)]
                    st2 = sb([128, 4, 2], F32, ots)
                    st2R = [R() for _ in range(4)]
                    outt = [(sb([128, D], F32, ots), R()) for _ in range(2)]
                    gi = 0
                    for bi in range(3):
                        for hf in range(2):
                            wy, wyR = wget("Y%d%d" % (bi, hf))
                            wg, wgR = wget("G%d%d" % (bi, hf))
                            for o4 in range(4):
                                oc = hf * 4 + o4
                                by = getps()
                                for k in range(4):
                                    K.op("pe", lambda e, by=by, k=k, o4=o4, bi=bi, wy=wy: e.matmul(
                                        PS[by][:, 0:512], lhsT=wy[:, k, o4 * 128:(o4 + 1) * 128], rhs=og[bi][:, k, :],
                                        start=(k == 0), stop=(k == 3)), reads=[wyR, ogR[bi][k]], writes=[PR[by]], sig=(k == 3))
                                bg = getps()
                                proj_fm(wg, wgR, o4 * 128, 128, tok0, 512, bg)
                                gs, gsR = gsig[gi % 2]
                                tf, tfR = tmpf[gi % 2]
                                gi += 1
                                K.op("act", lambda e, bg=bg, gs=gs: e.activation(out=gs[:, :], in_=PS[bg][:, 0:512], func=AF.Sigmoid),
                                     reads=[PR[bg]], writes=[gsR])
                                if bi == 0:
                                    K.op("dve", lambda e, by=by, gs=gs, oc=oc: e.tensor_tensor(
                                        out=merged[:, oc, :], in0=PS[by][:, 0:512], in1=gs[:, :], op=ALU.mult),
                                         reads=[PR[by], gsR], writes=[mR[oc]])
                                else:
                                    K.op("dve", lambda e, by=by, gs=gs, tf=tf: e.tensor_tensor(
                                        out=tf[:, :], in0=PS[by][:, 0:512], in1=gs[:, :], op=ALU.mult),
                                         reads=[PR[by], gsR], writes=[tfR])
                                    if bi == 1:
                                        K.op("pool", lambda e, tf=tf, oc=oc: e.tensor_tensor(
                                            out=merged[:, oc, :], in0=merged[:, oc, :], in1=tf[:, :], op=ALU.add),
                                             reads=[tfR, mR[oc]], writes=[mR[oc]])
                                    else:
                                        K.op("pool", lambda e, tf=tf, oc=oc: e.tensor_tensor(
                                            out=mbf[:, oc, :], in0=merged[:, oc, :], in1=tf[:, :], op=ALU.add),
                                             reads=[tfR, mR[oc]], writes=[mbR_[oc]])
                    wo0, wo0R = wget("O0")
                    wo1, wo1R = wget("O1")
                    for jj in range(4):
                        tt = 4 * G + jj
                        sidx = b * NT + tt
                        xb, xR = next_xbuf()
                        K.dma("sp", [(xb[:], x_d[b, tt * 128:(tt + 1) * 128, :])], writes=[xR])
                        zb = []
                        for half, (wo, woR) in enumerate(((wo0, wo0R), (wo1, wo1R))):
                            z = getps()
                            zb.append(z)
                            for kc in range(8):
                                K.op("pe", lambda e, z=z, kc=kc, jj=jj, wo=wo: e.matmul(
                                    PS[z][:, 0:512], lhsT=mbf[:, kc, jj * 128:(jj + 1) * 128], rhs=wo[:, kc, 0:512],
                                    start=(kc == 0), stop=(kc == 7)), reads=[woR, mbR_[kc]], writes=[PR[z]], sig=(kc == 7))
                        K.op("act", lambda e, xb=xb, sidx=sidx: e.activation(out=xb[:], in_=xb[:], func=AF.Identity,
                                                                              scale=stats[:, sidx, 1:2], bias=stats[:, sidx, 0:1]),
                             reads=[xR, statR[sidx]], writes=[xR])
                        t, tR = wk[jj % 2]
                        K.op("pool", lambda e, xb=xb, t=t: e.tensor_tensor(out=t[:], in0=xb[:], in1=ag[:], op=ALU.mult),
                             reads=[xR, cR3], writes=[tR])
                        K.op("pool", lambda e, t=t: e.tensor_tensor(out=t[:], in0=t[:], in1=ab[:], op=ALU.add),
                             reads=[tR, cR3], writes=[tR])
                        for half in range(2):
                            K.op("dve", lambda e, t=t, half=half, z=zb[half]: e.tensor_tensor(
                                out=t[:, half * 512:(half + 1) * 512], in0=PS[z][:, 0:512], in1=t[:, half * 512:(half + 1) * 512],
                                op=ALU.add), reads=[PR[zb[half]], tR], writes=[tR])
                        ln_stats(t, tR, st2[:, jj, :], st2R[jj])
                        ot, otR = outt[jj % 2]
                        K.op("act", lambda e, t=t, ot=ot, jj=jj: e.activation(out=ot[:], in_=t[:], func=AF.Identity,
                                                                               scale=st2[:, jj, 1:2], bias=st2[:, jj, 0:1]),
                             reads=[tR, st2R[jj]], writes=[otR])
                        K.op("pool", lambda e, ot=ot: e.tensor_tensor(out=ot[:], in0=ot[:], in1=lgb[:], op=ALU.mult),
                             reads=[otR, cR3], writes=[otR])
                        K.op("pool", lambda e, ot=ot: e.tensor_tensor(out=ot[:], in0=ot[:], in1=lbb[:], op=ALU.add),
                             reads=[otR, cR3], writes=[otR])
                        K.dma("sp", [(out_d[b, tt * 128:(tt + 1) * 128, :], ot[:])], reads=[otR])
                    K.barrier([otR_ for _, otR_ in outt])
        assert wst["got"] == len(specs)
    return nc


def _consts(S):
    pos = np.arange(S, dtype=np.float32)
    out = {}
    for hd, tag in ((64, "64"), (32, "32")):
        r = hd // 4
        half = r // 2
        inv = np.power(np.float32(500000.0), -np.arange(half, dtype=np.float32) * np.float32(2.0 / r)).astype(np.float32)
        ang = (pos[:, None] * inv[None, :]).astype(np.float32)
        cos = np.cos(ang).astype(np.float32)
        sin = np.sin(ang).astype(np.float32)
        c = np.ones((128, S), np.float32)
        s = np.zeros((128, S), np.float32)
        P = np.zeros((128, 128), np.float32)
        for p in range(128):
            i = p % hd
            if i < half:
                c[p] = cos[:, i]
                s[p] = sin[:, i]
                P[p, p + half] = 1.0
            elif i < r:
                c[p] = cos[:, i - half]
                s[p] = -sin[:, i - half]
                P[p, p - half] = 1.0
        out["cos" + tag] = c
        out["sp" + tag] = s
        out["perm" + tag] = P
    out["ident"] = np.eye(128, dtype=np.float32)
    return out


def make_in_maps(inputs, S, NB, ncores):
    f = lambda a: np.ascontiguousarray(np.asarray(a, dtype=np.float32))
    x = f(inputs["x"])
    mem = f(inputs["mem"])
    cst = _consts(S)
    shared = {
        "w_in": f(inputs["w_in"][0]), "w_mem_kv": f(inputs["w_mem_kv"][0]),
        "w_proj_a": f(inputs["w_proj_a"][0]), "w_proj_b": f(inputs["w_proj_b"][0]),
        "w_proj_c": f(inputs["w_proj_c"][0]), "w_out": f(inputs["w_out"][0]),
        "g_fm": f(np.asarray(inputs["ln_in_g"]).reshape(8, 128).T),
        "b_fm": f(np.asarray(inputs["ln_in_b"]).reshape(8, 128).T),
        "g_in_bc": f(np.broadcast_to(np.asarray(inputs["ln_in_g"]).reshape(1, D), (128, D))),
        "b_in_bc": f(np.broadcast_to(np.asarray(inputs["ln_in_b"]).reshape(1, D), (128, D))),
        "ln_g_bc": f(np.broadcast_to(np.asarray(inputs["ln_g"][0]).reshape(1, D), (128, D))),
        "ln_b_bc": f(np.broadcast_to(np.asarray(inputs["ln_b"][0]).reshape(1, D), (128, D))),
        "dlam_bc": f(np.broadcast_to(np.asarray(inputs["diff_lambda"][0]).reshape(1, 256), (128, 256))),
        "dng": f(np.asarray(inputs["diff_norm_g"][0]).reshape(128, 1)),
    }
    shared.update({k: f(v) for k, v in cst.items()})
    maps = []
    for c in range(ncores):
        m = dict(shared)
        m["x"] = f(x[c * NB:(c + 1) * NB])
        m["mem"] = f(mem[c * NB:(c + 1) * NB])
        maps.append(m)
    return maps


def kernel(**inputs):
    x = np.asarray(inputs["x"])
    B, S, _ = x.shape
    ncores = 8
    NB = B // ncores
    nc = build(S, NB)
    maps = make_in_maps(inputs, S, NB, ncores)
    res = run_bass_kernel_spmd(nc, maps, core_ids=list(range(ncores)))
    out = np.concatenate([np.asarray(r["out"]) for r in res.results], axis=0)
    return out.astype(np.float32)
```
